# Optimizing a Trainium2 kernel written in Bass

```python
import math
import jax, jax.numpy as jnp
from jax import lax
import numpy as np

D_MODEL = 4096
BATCH = 4
SEQ = 4096
DEPTH = 1

HEAD_DIM = 128
ATTN_WIDTH = D_MODEL // 2
CONV_WIDTH = D_MODEL - ATTN_WIDTH
MIX_WIDTH = ATTN_WIDTH + CONV_WIDTH
N_Q_HEADS = ATTN_WIDTH // HEAD_DIM
N_KV_HEADS = max(1, N_Q_HEADS // 4)
GQA_GROUP = N_Q_HEADS // N_KV_HEADS
KV_WIDTH = N_KV_HEADS * HEAD_DIM
WINDOW = 128
ATTN_BLOCK = 128
ROPE_THETA = 500000.0
ROPE_DIM = HEAD_DIM // 4
CONV_WIDTH_T = 31
NORM_EPS = 1e-6
LN_EPS = 1e-5
IN_SIZES = (ATTN_WIDTH, KV_WIDTH, KV_WIDTH, ATTN_WIDTH, 2 * CONV_WIDTH, CONV_WIDTH)
IN_WIDTH = sum(IN_SIZES)
SPLITS = tuple(int(v) for v in np.cumsum(IN_SIZES)[:-1])

kernel_name = "hymba_swa_conformer_conv_encoder_layer"


def rms_norm(x, gain, eps):
    x32 = x.astype(jnp.float32)
    y = x32 * lax.rsqrt(jnp.mean(x32 * x32, axis=-1, keepdims=True) + eps)
    return (y * gain.astype(jnp.float32)).astype(x.dtype)


def layer_norm(x, gain, bias, eps):
    x32 = x.astype(jnp.float32)
    mu = jnp.mean(x32, axis=-1, keepdims=True)
    xc = x32 - mu
    var = jnp.mean(xc * xc, axis=-1, keepdims=True)
    y = xc * lax.rsqrt(var + eps) * gain.astype(jnp.float32) + bias.astype(jnp.float32)
    return y.astype(x.dtype)


def partial_rope(x, pos):
    half = ROPE_DIM // 2
    inv_freq = jnp.power(jnp.float32(ROPE_THETA), -jnp.arange(half, dtype=jnp.float32) * (2.0 / ROPE_DIM))
    ang = pos.astype(jnp.float32)[:, None] * inv_freq[None, :]
    cos = jnp.cos(ang)[:, None, :]
    sin = jnp.sin(ang)[:, None, :]
    x32 = x.astype(jnp.float32)
    x1, x2, rest = x32[..., :half], x32[..., half:ROPE_DIM], x32[..., ROPE_DIM:]
    out = jnp.concatenate([x1 * cos - x2 * sin, x2 * cos + x1 * sin, rest], axis=-1)
    return out.astype(x.dtype)


def banded_window_attention(q, k, v, sink):
    B, S = q.shape[0], q.shape[1]
    nb = S // ATTN_BLOCK
    qb = q.reshape(B, nb, ATTN_BLOCK, N_KV_HEADS, GQA_GROUP, HEAD_DIM)

    def neighbours(t):
        tp = jnp.pad(t, ((0, 0), (ATTN_BLOCK, ATTN_BLOCK), (0, 0), (0, 0)))
        tb = tp.reshape(B, nb + 2, ATTN_BLOCK, N_KV_HEADS, HEAD_DIM)
        return jnp.concatenate([tb[:, :-2], tb[:, 1:-1], tb[:, 2:]], axis=2)

    kb, vb = neighbours(k), neighbours(v)
    scale = 1.0 / math.sqrt(HEAD_DIM)
    scores = jnp.einsum('bnqkgd,bnskd->bnkgqs', qb, kb,
                        preferred_element_type=jnp.float32) * scale
    blk_idx = jnp.arange(nb)[:, None] * ATTN_BLOCK
    q_pos = blk_idx + jnp.arange(ATTN_BLOCK)[None, :]
    k_pos = blk_idx - ATTN_BLOCK + jnp.arange(3 * ATTN_BLOCK)[None, :]
    rel = k_pos[:, None, :] - q_pos[:, :, None]
    valid = (jnp.abs(rel) <= WINDOW) & (k_pos[:, None, :] >= 0) & (k_pos[:, None, :] < S)
    scores = jnp.where(valid[None, :, None, None, :, :], scores, -jnp.inf)
    sink_b = sink.astype(jnp.float32).reshape(1, 1, N_KV_HEADS, GQA_GROUP, 1, 1)
    m = jnp.maximum(jnp.max(scores, axis=-1, keepdims=True), sink_b)
    p = jnp.exp(scores - m)
    denom = jnp.sum(p, axis=-1, keepdims=True) + jnp.exp(sink_b - m)
    probs = (p / denom).astype(v.dtype)
    out = jnp.einsum('bnkgqs,bnskd->bnqkgd', probs, vb)
    return out.reshape(B, S, N_Q_HEADS * HEAD_DIM)


def conformer_conv(u, dw_w, dw_b, ln_g, ln_b, pw_w, pw_b):
    a, b = jnp.split(u, 2, axis=-1)
    h = a * jax.nn.sigmoid(b)
    pad = CONV_WIDTH_T // 2
    h = lax.conv_general_dilated(h, dw_w[:, None, :], window_strides=(1,),
                                 padding=[(pad, pad)],
                                 dimension_numbers=('NWC', 'WIO', 'NWC'),
                                 feature_group_count=CONV_WIDTH) + dw_b
    h = jax.nn.silu(layer_norm(h, ln_g, ln_b, LN_EPS))
    return jnp.einsum('bsc,ce->bse', h, pw_w) + pw_b


def setup_inputs(seed: int = 0) -> dict:
    key = jax.random.key(seed)
    ks = jax.random.split(key, 14)
    f32 = jnp.float32
    nrm = lambda k, shape: jax.random.normal(k, shape, f32)
    return {
        "x": nrm(ks[0], (BATCH, SEQ, D_MODEL)),
        "norm_gain": 1.0 + 0.02 * nrm(ks[1], (DEPTH, D_MODEL)),
        "w_in": nrm(ks[2], (DEPTH, D_MODEL, IN_WIDTH)) * D_MODEL ** -0.5,
        "q_norm_gain": 1.0 + 0.02 * nrm(ks[3], (DEPTH, HEAD_DIM)),
        "k_norm_gain": 1.0 + 0.02 * nrm(ks[4], (DEPTH, HEAD_DIM)),
        "attn_sink": 0.5 * nrm(ks[5], (DEPTH, N_Q_HEADS)),
        "conv_dw_w": nrm(ks[6], (DEPTH, CONV_WIDTH_T, CONV_WIDTH)) * CONV_WIDTH_T ** -0.5,
        "conv_dw_b": 0.02 * nrm(ks[7], (DEPTH, CONV_WIDTH)),
        "conv_ln_gain": 1.0 + 0.02 * nrm(ks[8], (DEPTH, CONV_WIDTH)),
        "conv_ln_bias": 0.02 * nrm(ks[9], (DEPTH, CONV_WIDTH)),
        "conv_pw_w": nrm(ks[10], (DEPTH, CONV_WIDTH, CONV_WIDTH)) * CONV_WIDTH ** -0.5,
        "conv_pw_b": 0.02 * nrm(ks[11], (DEPTH, CONV_WIDTH)),
        "w_out": nrm(ks[12], (DEPTH, MIX_WIDTH, D_MODEL)) * MIX_WIDTH ** -0.5,
    }


def reference(x, norm_gain, w_in, q_norm_gain, k_norm_gain, attn_sink, conv_dw_w, conv_dw_b,
              conv_ln_gain, conv_ln_bias, conv_pw_w, conv_pw_b, w_out):
    B, S = x.shape[0], x.shape[1]
    pos = jnp.arange(S)
    for layer in range(DEPTH):
        xn = rms_norm(x, norm_gain[layer], NORM_EPS)
        proj = jnp.einsum('bsd,de->bse', xn, w_in[layer])
        q, k, v, g_attn, u_conv, g_conv = jnp.split(proj, SPLITS, axis=-1)
        q = q.reshape(B, S, N_Q_HEADS, HEAD_DIM)
        k = k.reshape(B, S, N_KV_HEADS, HEAD_DIM)
        v = v.reshape(B, S, N_KV_HEADS, HEAD_DIM)
        q = partial_rope(rms_norm(q, q_norm_gain[layer], NORM_EPS), pos)
        k = partial_rope(rms_norm(k, k_norm_gain[layer], NORM_EPS), pos)
        attn = banded_window_attention(q, k, v, attn_sink[layer]) * jax.nn.silu(g_attn)
        conv = conformer_conv(u_conv, conv_dw_w[layer], conv_dw_b[layer], conv_ln_gain[layer],
                              conv_ln_bias[layer], conv_pw_w[layer], conv_pw_b[layer]) * jax.nn.silu(g_conv)
        mixed = jnp.concatenate([attn, conv], axis=-1)
        x = x + jnp.einsum('bsm,md->bsd', mixed, w_out[layer])
    return x
```

```python
import math
from collections import deque
from contextlib import ExitStack

import numpy as np
import concourse.bass as bass
import concourse.mybir as mybir
from concourse.bass_utils import run_bass_kernel_spmd

F32 = mybir.dt.float32
BF16 = mybir.dt.bfloat16
AF = mybir.ActivationFunctionType
ALU = mybir.AluOpType

D_MODEL = 4096
SEQ = 4096
BATCH = 4
NCORES = 8
TOK = 2048
EXT = TOK + 256
TT = 512
NTILE = TOK // TT
TW = TT + 256
NSLOT = 4
NSLAB_W = 128
CAST_BOUNDS = list(range(0, 16, 2)) + list(range(16, 129, 8))
NCASTG = len(CAST_BOUNDS) - 1


def _cast_group(n):
    for g in range(NCASTG):
        if CAST_BOUNDS[g] <= n < CAST_BOUNDS[g + 1]:
            return g
    raise ValueError(n)
NORM_EPS = 1e-6
LN_EPS = 1e-5
ROPE_THETA = 500000.0
SCALE = 1.0 / math.sqrt(128.0)

C_ID = 0
C_MASK = 128
C_SW = 640
C_GAIN = 672
C_QKG = 704
C_SINK = 706
C_DW = 722
C_DWB = 1218
C_LNG = 1234
C_LNB = 1250
C_PWB = 1266
C_EPS = 1282
NCST = 1284


class T:
    __slots__ = ("w", "r", "war")

    def __init__(self):
        self.w = {}
        self.r = {}
        self.war = {}


def _merge(d, s):
    for k, v in s.items():
        if d.get(k, 0) < v:
            d[k] = v


class Sched:
    def __init__(self):
        self.streams = {e: [] for e in ("pe", "act", "dve", "pool", "sp")}
        self.cnt = {}
        self.waited = {e: {} for e in self.streams}
        self.sem = {}

    def new_sem(self, name):
        self.cnt[name] = 0

    def emit(self, eng, fn, deps, sig=None, inc=1):
        waits = []
        wd = self.waited[eng]
        for s, v in deps.items():
            if eng == "pe" and s == "pe":
                continue
            if wd.get(s, 0) >= v:
                continue
            wd[s] = v
            waits.append((s, v))
        h = {}
        if sig is not None:
            self.cnt[sig] += inc
            h = {sig: self.cnt[sig]}
        self.streams[eng].append((waits, fn, sig, inc))
        return h

    def op(self, eng, fn, reads=(), writes=(), partial=(), sig="auto", inc=1):
        deps = {}
        for t in reads:
            _merge(deps, t.w)
        for t in writes:
            war = dict(t.w)
            _merge(war, t.r)
            t.war = war
            t.w = {}
            t.r = {}
            _merge(deps, war)
        for t in partial:
            _merge(deps, t.war)
        if sig == "auto":
            sig = eng
        h = self.emit(eng, fn, deps, sig, inc)
        for t in reads:
            _merge(t.r, h)
        for t in writes:
            _merge(t.w, h)
        for t in partial:
            _merge(t.w, h)
        return h

    def mm_group(self, fns, reads=(), writes=(), partial=()):
        deps = {}
        for t in reads:
            _merge(deps, t.w)
        for t in writes:
            war = dict(t.w)
            _merge(war, t.r)
            t.war = war
            t.w = {}
            t.r = {}
            _merge(deps, war)
        for t in partial:
            _merge(deps, t.war)
        n = len(fns)
        h = {}
        for i, fn in enumerate(fns):
            h = self.emit("pe", fn, deps if i == 0 else {}, "pe" if i == n - 1 else None)
        for t in reads:
            _merge(t.r, h)
        for t in writes:
            _merge(t.w, h)
        for t in partial:
            _merge(t.w, h)
        return h


class Ring:
    def __init__(self, aps):
        self.aps = aps
        self.ts = [T() for _ in aps]
        self.held = [0] * len(aps)
        self.i = 0

    def next(self, hold=False):
        for _ in range(2 * len(self.aps)):
            i = self.i
            self.i = (i + 1) % len(self.aps)
            if not self.held[i]:
                if hold:
                    self.held[i] = 1
                return self.aps[i], self.ts[i]
        raise RuntimeError("tmp ring exhausted")

    def release(self, ap):
        for i, a in enumerate(self.aps):
            if a is ap:
                self.held[i] = 0
                return
        raise RuntimeError("release of unknown buffer")


def build_program(debug=False):
    nc = bass.Bass("TRN2", target_bir_lowering=False)
    x_ext = nc.dram_tensor("x_ext", [EXT, D_MODEL], F32, kind="ExternalInput").ap()
    wall = nc.dram_tensor("wall", [NSLAB_W, 128, 4096], F32, kind="ExternalInput").ap()
    cst_d = nc.dram_tensor("cst", [128, NCST], F32, kind="ExternalInput").ap()
    rope_d = nc.dram_tensor("rope", [32, 2, EXT], F32, kind="ExternalInput").ap()
    out_d = nc.dram_tensor("out", [TOK, D_MODEL], F32, kind="ExternalOutput").ap()
    wbf = nc.dram_tensor("wbf", [NSLAB_W, 128, 4096], BF16, kind="Internal").ap()
    dgd = nc.dram_tensor("dgd", [16, 128, 31 * 128], BF16, kind="Internal").ap()
    if debug:
        dbg_mix = nc.dram_tensor("dbg_mix", [128, 32 * TT], BF16, kind="ExternalOutput").ap()
        dbg_xT = nc.dram_tensor("dbg_xT", [128, 32 * TW], BF16, kind="ExternalOutput").ap()
        dbg_kT = nc.dram_tensor("dbg_kT", [128, 4 * TW], BF16, kind="ExternalOutput").ap()
        dbg_v = nc.dram_tensor("dbg_v", [128, 6 * 512], BF16, kind="ExternalOutput").ap()
        dbg_q = nc.dram_tensor("dbg_q", [128, 2 * 4 * TT], BF16, kind="ExternalOutput").ap()

    S = Sched()
    es = ExitStack()
    with es:
        def sb(name, shape, dt):
            return es.enter_context(nc.sbuf_tensor(name, shape, dt))

        xT = sb("xT", [128, 32, TW], BF16)
        mixT = sb("mixT", [128, 32, TT], BF16)
        kT = sb("kT", [128, 4, TW], BF16)
        v_sb = sb("v_sb", [128, 6, 512], BF16)
        qT = sb("qT", [128, 2, 4, TT], BF16)
        xb = sb("xb", [128, D_MODEL], F32)
        xs = sb("xs", [128, 2, 512], F32)
        wring = [sb(f"wring{i}", [128, 4096], BF16) for i in range(NSLOT)]
        rope_t = sb("rope_t", [32, 2, TW], F32)
        NTF, NTB = 8, 12
        tmpF = Ring([sb(f"tmpF{i}", [128, 544], F32) for i in range(NTF)])
        tmpB = Ring([sb(f"tmpB{i}", [128, 544], BF16) for i in range(NTB)])
        xres = sb("xres", [128, 4, 512], F32)
        cst = sb("cst_sb", [128, NCST], F32)
        ident_bf = sb("ident_bf", [128, 128], BF16)
        ones_bf = sb("ones_bf", [128, 128], BF16)
        sw_bf = sb("sw_bf", [32, 32], BF16)
        masks_bf = sb("masks_bf", [128, 4, 128], BF16)
        esink = sb("esink", [128, 16], F32)
        small = sb("small", [128, 8], F32)
        ps = [es.enter_context(nc.psum_tensor(f"ps{i}", [128, 512], F32)) for i in range(8)]

        sem_names = ["pe", "act", "dve", "pool", "cst", "xb", "rope", "xres", "ost", "dg"]
        sem_names += [f"w{i}" for i in range(NSLOT)]
        sem_names += [f"cast{i}" for i in range(NCASTG)]
        for n in sem_names:
            S.sem[n] = es.enter_context(nc.semaphore(n))
            S.new_sem(n)

        xT_T = [T() for _ in range(6)]
        mix_T = [T() for _ in range(32)]
        kT_T = [[T(), T()] for _ in range(4)]
        v_T = [[T(), T()] for _ in range(4)]
        q_T = [[T() for _ in range(4)] for _ in range(2)]
        xb_T, rope_T, xres_T = T(), T(), T()
        xs_T = [T(), T()]
        slot_T = [T() for _ in range(NSLOT)]
        cst_T, esink_T = T(), T()
        small_T = [T() for _ in range(8)]
        bank_T = [T() for _ in range(8)]
        cast_T = [T() for _ in range(NCASTG)]
        dgd_T = [T() for _ in range(16)]
        out_T = T()
        busy = [False] * 8
        rr = [0]

        def alloc_bank(hold=False):
            for _ in range(16):
                i = rr[0]
                rr[0] = (i + 1) % 8
                if not busy[i]:
                    if hold:
                        busy[i] = True
                    return i
            raise RuntimeError("no free PSUM bank")

        def act(out, in_, func, reads, writes=(), partial=(), scale=1.0, bias=None, accum_out=None):
            kw = {}
            if bias is not None:
                kw["bias"] = bias
            if accum_out is not None:
                kw["accum_out"] = accum_out
            return S.op("act", lambda e: e.activation(out=out, in_=in_, func=func, scale=scale, **kw),
                        reads, writes, partial)

        def tt(eng, out, in0, in1, op, reads, writes=(), partial=()):
            return S.op(eng, lambda e: e.tensor_tensor(out=out, in0=in0, in1=in1, op=op),
                        reads, writes, partial)

        def ts(eng, out, in0, s1, s2, op0, op1, reads, writes=(), partial=()):
            return S.op(eng, lambda e: e.tensor_scalar(out=out, in0=in0, scalar1=s1, scalar2=s2, op0=op0, op1=op1),
                        reads, writes, partial)

        def stt(out, in0, scalar, in1, op0, op1, reads, writes=(), partial=()):
            return S.op("dve", lambda e: e.scalar_tensor_tensor(out=out, in0=in0, scalar=scalar, in1=in1,
                                                               op0=op0, op1=op1),
                        reads, writes, partial)

        def dma(q, out, in_, reads, writes, sem):
            return S.op(q, lambda e: e.dma_start(out=out, in_=in_), reads, writes, (), sig=sem, inc=16)

        def mm(out, lhsT, rhs, start, stop):
            return lambda e: e.matmul(out, lhsT=lhsT, rhs=rhs, start=start, stop=stop)

        pending = deque()
        mcount = [0]

        def defer(fn, lag):
            rel = mcount[0] + lag
            idx = len(pending)
            for k, (r, _) in enumerate(pending):
                if r > rel:
                    idx = k
                    break
            pending.insert(idx, (rel, fn))

        def flush(upto=None):
            while pending and (upto is None or pending[0][0] <= upto):
                _, fn = pending.popleft()
                fn()

        def main_done():
            mcount[0] += 1
            flush(mcount[0])

        tile_stream = []
        sid = 0
        for i in range(16):
            tile_stream += [("w", sid), ("w", sid + 1), ("dg", i)]
            sid += 2
        for _ in range(8):
            tile_stream.append(("w", sid))
            sid += 1
        for _ in range(24):
            tile_stream.append(("w", sid))
            sid += 1
        for _ in range(32):
            tile_stream.append(("w", sid))
            sid += 1
        for _ in range(32):
            tile_stream.append(("w", sid))
            sid += 1
        assert sid == NSLAB_W
        full_stream = tile_stream * NTILE
        ld = {"next_load": 0, "next_use": 0}
        slot_hold = [0] * NSLOT
        wb_needed = {}

        def issue_load(force=False):
            while True:
                n = ld["next_load"]
                if n >= len(full_stream):
                    return False
                if not slot_hold[n % NSLOT]:
                    break
                if not force:
                    return False
                assert pending, "held slot without pending consumer"
                _, fn = pending.popleft()
                fn()
                return True
            ld["next_load"] = n + 1
            kind, idx = full_stream[n]
            slot = n % NSLOT
            tile_i = n // len(tile_stream)
            if kind == "w" and (tile_i == 0 or (tile_i == 1 and idx % 2 == 1)):
                g = _cast_group(idx)
                dma("pool", wring[slot][:, :], wall[idx], [], [slot_T[slot]], f"w{slot}")
                if tile_i == idx % 2:
                    wb_needed[n] = (idx, g)
            elif kind == "w":
                dma("sp", wring[slot][:, :], wbf[idx], [cast_T[_cast_group(idx)]], [slot_T[slot]], f"w{slot}")
            else:
                dma("sp", wring[slot][:, 0:31 * 128], dgd[idx], [dgd_T[idx]], [slot_T[slot]], f"w{slot}")
            return True

        def next_slab():
            n = ld["next_use"]
            ld["next_use"] = n + 1
            while ld["next_load"] <= n:
                issue_load(force=True)
            if n in wb_needed:
                idx, g = wb_needed.pop(n)
                hh = dma("sp", wbf[idx], wring[n % NSLOT][:, :], [slot_T[n % NSLOT]], (), f"cast{g}")
                _merge(cast_T[g].w, hh)
            return n % NSLOT

        def prefetch():
            while ld["next_load"] < min(len(full_stream), ld["next_use"] + NSLOT - 1):
                if not issue_load():
                    break

        dma("sp", cst[:, :], cst_d, [], [cst_T], "cst")
        ones_T = T()
        S.op("dve", lambda e: e.memset(ones_bf[:, :], 1.0), [], [ones_T])
        S.op("dve", lambda e: e.tensor_copy(out=ident_bf[:, :], in_=cst[:, C_ID:C_ID + 128]), [cst_T], [ones_T])
        S.op("dve", lambda e: e.tensor_copy(out=sw_bf[:, :], in_=cst[0:32, C_SW:C_SW + 32]), [cst_T], [ones_T])
        S.op("dve", lambda e: e.tensor_copy(out=masks_bf[:, :, :],
                                            in_=cst[:, C_MASK:C_MASK + 512].rearrange("p (a b) -> p a b", a=4)),
             [cst_T], [ones_T])
        act(esink[:, :], cst[:, C_SINK:C_SINK + 16], AF.Exp, [cst_T], [esink_T])

        def build_diag(i):
            q4 = i % 4
            stage = mixT[:, 8 * q4:8 * q4 + 8, :].rearrange("p a b -> p (a b)")
            st_T = mix_T[8 * q4:8 * q4 + 8]
            for j in range(31):
                c = C_DW + i * 31 + j
                S.op("dve", lambda e, j=j, c=c, stage=stage: e.tensor_scalar(
                    out=stage[:, j * 128:(j + 1) * 128], in0=ident_bf[:, :],
                    scalar1=cst[:, c:c + 1], scalar2=0.0, op0=ALU.mult, op1=ALU.add),
                    [cst_T, ones_T], st_T if j == 0 else (), () if j == 0 else st_T)
            dma("sp", dgd[i], stage[:, 0:31 * 128], st_T, [dgd_T[i]], "dg")

        eps_n = cst[:, C_EPS:C_EPS + 1]
        eps_l = cst[:, C_EPS + 1:C_EPS + 2]

        def xprep_plan(j):
            loads, works = [], []
            for blk in range(6):
                def load(blk=blk):
                    r0 = TT * j + 128 * blk
                    dma("sp", xb[:, :], x_ext[r0:r0 + 128, :], [], [xb_T], "xb")
                    junk = xs[:, :, :].rearrange("p a b -> p (a b)").bitcast(BF16)[:, 0:2048]
                    act(junk, xb[:, 0:2048], AF.Square, [xb_T], [xs_T[0], xs_T[1], small_T[0]],
                        accum_out=small[:, 0:1])
                    act(junk, xb[:, 2048:4096], AF.Square, [xb_T], [xs_T[0], xs_T[1], small_T[1]],
                        accum_out=small[:, 1:2])
                    tt("dve", small[:, 2:3], small[:, 0:1], small[:, 1:2], ALU.add,
                       [small_T[0], small_T[1]], [small_T[2]])
                    act(small[:, 3:4], small[:, 2:3], AF.Ln, [small_T[2], cst_T], [small_T[3]],
                        scale=1.0 / D_MODEL, bias=eps_n)
                    act(small[:, 4:5], small[:, 3:4], AF.Exp, [small_T[3]], [small_T[4]], scale=-0.5)
                loads.append(load)
                ws = []
                for c in range(8):
                    def work(blk=blk, c=c):
                        h = c % 2
                        act(xs[:, h, :], xb[:, c * 512:(c + 1) * 512], AF.Copy, [xb_T, small_T[4]],
                            [xs_T[h]], scale=small[:, 4:5])
                        b = alloc_bank()
                        fns = []
                        for kk in range(4):
                            fns.append(lambda e, kk=kk, b=b, h=h: e.transpose(
                                ps[b][:, kk * 128:(kk + 1) * 128], xs[:, h, kk * 128:(kk + 1) * 128],
                                cst[:, C_ID:C_ID + 128]))
                        S.mm_group(fns, [xs_T[h], cst_T], [bank_T[b]])
                        kc0 = c * 4
                        first = (c == 0)
                        tt("dve", xT[:, kc0:kc0 + 4, blk * 128:(blk + 1) * 128],
                           ps[b][:, :].rearrange("p (a b) -> p a b", a=4),
                           cst[:, C_GAIN + kc0:C_GAIN + kc0 + 4].unsqueeze(2).broadcast_to([128, 4, 128]),
                           ALU.mult, [bank_T[b], cst_T],
                           [xT_T[blk]] if first else (), () if first else [xT_T[blk]])
                    ws.append(work)
                works.append(ws)
            return loads, works

        def xprep_slots(j):
            loads, works = xprep_plan(j)
            slots = [[] for _ in range(32)]
            for k in range(6):
                for i in range(4):
                    slots[5 * k + i] += [works[k][2 * i], works[k][2 * i + 1]]
                if k + 1 < 6:
                    slots[5 * k + 3].append(loads[k + 1])
            return loads[0], slots

        def xT_reads(t0, n):
            return [xT_T[b] for b in range(t0 // 128, (t0 + n - 1) // 128 + 1)]

        def inproj(slot, t0, n, b, col0=0):
            fns = []
            for kc in range(32):
                fns.append(mm(ps[b][:, col0:col0 + n], wring[slot][:, kc * 128:(kc + 1) * 128],
                              xT[:, kc, t0:t0 + n], kc == 0, kc == 31))
            S.mm_group(fns, [slot_T[slot]] + xT_reads(t0, n), [bank_T[b]])

        def norm_rope(b, n, gcol, tok0, dest, dest_T, lag3=1):
            sq, sq_T = tmpB.next(hold=True)
            qg, qg_T = tmpF.next(hold=True)
            act(sq[:, 0:n], ps[b][:, 0:n], AF.Square, [bank_T[b]], [sq_T])
            act(qg[:, 0:n], ps[b][:, 0:n], AF.Copy, [bank_T[b], cst_T], [qg_T],
                scale=cst[:, C_QKG + gcol:C_QKG + gcol + 1])

            def stage2():
                b2 = alloc_bank()
                S.mm_group([mm(ps[b2][:, 0:n], ones_bf[:, :], sq[:, 0:n], True, True)],
                           [ones_T, sq_T], [bank_T[b2]])
                r, r_T = tmpF.next()
                act(r[:, 0:n], ps[b2][:, 0:n], AF.Ln, [bank_T[b2], cst_T], [r_T], scale=1.0 / 128.0, bias=eps_n)
                act(r[:, 0:n], r[:, 0:n], AF.Exp, [r_T], [r_T], scale=-0.5)
                tt("dve", dest, qg[:, 0:n], r[:, 0:n], ALU.mult, [qg_T, r_T], [dest_T])
                tmpB.release(sq)
                tmpF.release(qg)

                def stage3():
                    b3 = alloc_bank()
                    S.mm_group([mm(ps[b3][0:32, 0:n], sw_bf[:, :], dest[0:32], True, True)],
                               [ones_T, dest_T], [bank_T[b3]])
                    t1, t1_T = tmpF.next()
                    t2, t2_T = tmpF.next()
                    tt("dve", t1[0:32, 0:n], dest[0:32], rope_t[:, 0, tok0:tok0 + n], ALU.mult,
                       [dest_T, rope_T], [t1_T])
                    tt("dve", t2[0:32, 0:n], ps[b3][0:32, 0:n], rope_t[:, 1, tok0:tok0 + n], ALU.mult,
                       [bank_T[b3], rope_T], [t2_T])
                    tt("dve", dest[0:32], t1[0:32, 0:n], t2[0:32, 0:n], ALU.add, [t1_T, t2_T, dest_T], [dest_T])
                defer(stage3, lag3)
            defer(stage2, 1)

        ld0, slots0 = xprep_slots(0)
        ld0()
        build_diag(0)
        build_diag(1)
        nd = 2
        for sl in slots0:
            for st in sl:
                st()
            if nd < 16 and sl:
                build_diag(nd)
                nd += 1
        while nd < 16:
            build_diag(nd)
            nd += 1
        store_count = 0
        for j in range(NTILE):
            dma("sp", rope_t[:, :, :], rope_d[:, :, TT * j:TT * j + TW], [], [rope_T], "rope")

            for i in range(16):
                sa = next_slab()
                a_sb, a_T = tmpF.next(hold=True)
                for s in range(2):
                    b = alloc_bank()
                    inproj(sa, 112 + 272 * s, 272, b)
                    act(a_sb[:, 272 * s:272 * s + 272], ps[b][:, 0:272], AF.Copy, [bank_T[b]],
                        [a_T] if s == 0 else (), () if s == 0 else [a_T])
                main_done()
                prefetch()
                sbb = next_slab()
                sg, sg_T = tmpF.next(hold=True)
                for s in range(2):
                    b = alloc_bank()
                    inproj(sbb, 112 + 272 * s, 272, b)
                    act(sg[:, 272 * s:272 * s + 272], ps[b][:, 0:272], AF.Sigmoid, [bank_T[b]],
                        [sg_T] if s == 0 else (), () if s == 0 else [sg_T])
                h_sb, h_T = tmpB.next(hold=True)
                tt("dve", h_sb[:, :], a_sb[:, :], sg[:, :], ALU.mult, [a_T, sg_T], [h_T])
                tmpF.release(a_sb)
                tmpF.release(sg)
                main_done()
                sd = next_slab()
                slot_hold[sd] += 1

                def conv(i=i, sd=sd, h_sb=h_sb, h_T=h_T):
                    b = alloc_bank()
                    fns = [mm(ps[b][:, :], wring[sd][:, jj * 128:(jj + 1) * 128], h_sb[:, 1 + jj:1 + jj + 512],
                              jj == 0, jj == 30) for jj in range(31)]
                    S.mm_group(fns, [slot_T[sd], h_T], [bank_T[b]])
                    act(mixT[:, i, :], ps[b][:, :], AF.Identity, [bank_T[b], cst_T], [mix_T[i]],
                        bias=cst[:, C_DWB + i:C_DWB + i + 1])
                    slot_hold[sd] -= 1
                    tmpB.release(h_sb)
                    prefetch()
                defer(conv, 1)

            def ln_stats():
                bs = alloc_bank(hold=True)
                bq = alloc_bank(hold=True)
                S.mm_group([mm(ps[bs][:, :], ones_bf[:, :], mixT[:, i, :], i == 0, i == 15) for i in range(16)],
                           [ones_T] + mix_T[0:16], [bank_T[bs]])
                for i in range(16):
                    sq, sq_T = tmpB.next()
                    act(sq[:, 0:512], mixT[:, i, :], AF.Square, [mix_T[i]], [sq_T])
                    S.mm_group([mm(ps[bq][:, :], ones_bf[:, :], sq[:, 0:512], i == 0, i == 15)],
                               [ones_T, sq_T], [bank_T[bq]] if i == 0 else (), () if i == 0 else [bank_T[bq]])
                mean, mean_T = tmpF.next(hold=True)
                var, var_T = tmpF.next(hold=True)
                rs, rs_T = tmpF.next(hold=True)
                n = 512
                act(mean[:, 0:n], ps[bs][:, :], AF.Copy, [bank_T[bs]], [mean_T], scale=1.0 / 2048.0)
                tt("dve", var[:, 0:n], mean[:, 0:n], mean[:, 0:n], ALU.mult, [mean_T], [var_T])
                stt(var[:, 0:n], ps[bq][:, :], 1.0 / 2048.0, var[:, 0:n], ALU.mult, ALU.subtract,
                    [bank_T[bq], var_T], [var_T])
                busy[bs] = False
                busy[bq] = False
                act(rs[:, 0:n], var[:, 0:n], AF.Ln, [var_T, cst_T], [rs_T], bias=eps_l)
                act(rs[:, 0:n], rs[:, 0:n], AF.Exp, [rs_T], [rs_T], scale=-0.5)
                stt(mean[:, 0:n], mean[:, 0:n], -1.0, rs[:, 0:n], ALU.mult, ALU.mult, [mean_T, rs_T], [mean_T])
                for i in range(16):
                    z, z_T = tmpF.next()
                    tt("dve", z[:, 0:n], mixT[:, i, :], rs[:, 0:n], ALU.mult, [mix_T[i], rs_T], [z_T])
                    tt("dve", z[:, 0:n], z[:, 0:n], mean[:, 0:n], ALU.add, [z_T, mean_T], [z_T])
                    act(mixT[:, i, :], z[:, 0:n], AF.Silu, [z_T, cst_T], [mix_T[i]],
                        scale=cst[:, C_LNG + i:C_LNG + i + 1], bias=cst[:, C_LNB + i:C_LNB + i + 1])
                tmpF.release(mean)
                tmpF.release(var)
                tmpF.release(rs)
            defer(ln_stats, 2)

            for g in range(4):
                sv = next_slab()
                for half, blks in ((0, (0, 1, 2, 3)), (1, (4, 5))):
                    b = alloc_bank()
                    fns = []
                    for bi, blk in enumerate(blks):
                        for kc in range(32):
                            fns.append(mm(ps[b][:, bi * 128:(bi + 1) * 128], xT[:, kc, blk * 128:(blk + 1) * 128],
                                          wring[sv][:, kc * 128:(kc + 1) * 128], kc == 0, kc == 31))
                    S.mm_group(fns, [slot_T[sv]] + [xT_T[blk] for blk in blks], [bank_T[b]])
                    nb = len(blks)
                    S.op("dve", lambda e, b=b, nb=nb, blks=blks, g=g: e.tensor_copy(
                        out=v_sb[:, blks[0]:blks[0] + nb, g * 128:(g + 1) * 128],
                        in_=ps[b][:, 0:nb * 128].rearrange("p (a b) -> p a b", a=nb)),
                        [bank_T[b]], [v_T[g][half]])
                main_done()
                prefetch()
            for kh in range(4):
                sk = next_slab()
                for s in range(2):
                    b = alloc_bank()
                    inproj(sk, 384 * s, 384, b)
                    norm_rope(b, 384, 1, 384 * s, kT[:, kh, 384 * s:384 * s + 384], kT_T[kh][s], lag3=3)
                main_done()
                prefetch()

            for k in range(8):
                gates = []
                for el in range(2):
                    sgc_slot = next_slab()
                    b = alloc_bank()
                    inproj(sgc_slot, 128, 512, b)
                    sgc, sgc_T = tmpB.next(hold=True)
                    act(sgc[:, 0:512], ps[b][:, :], AF.Silu, [bank_T[b]], [sgc_T])
                    gates.append((sgc, sgc_T))
                    main_done()
                    prefetch()
                spw = next_slab()
                slot_hold[spw] += 2
                for el in range(2):
                    e_ = 2 * k + el

                    def pw(e_=e_, el=el, spw=spw, sgc=gates[el][0], sgc_T=gates[el][1]):
                        b2 = alloc_bank()
                        fns = [mm(ps[b2][:, :], wring[spw][:, (el * 16 + cc) * 128:(el * 16 + cc + 1) * 128],
                                  mixT[:, cc, :], cc == 0, cc == 15) for cc in range(16)]
                        S.mm_group(fns, [slot_T[spw]] + mix_T[0:16], [bank_T[b2]])
                        stt(mixT[:, 16 + e_, :], ps[b2][:, :], cst[:, C_PWB + e_:C_PWB + e_ + 1], sgc[:, 0:512],
                            ALU.add, ALU.mult, [bank_T[b2], cst_T, sgc_T], [mix_T[16 + e_]])
                        slot_hold[spw] -= 1
                        tmpB.release(sgc)
                        prefetch()
                    defer(pw, 1)

            for kh in range(4):
                qb_ = kh % 2
                for hh in range(4):
                    sq_slot = next_slab()
                    b = alloc_bank()
                    inproj(sq_slot, 128, 512, b)
                    norm_rope(b, 512, 0, 128, qT[:, qb_, hh, :], q_T[qb_][hh])
                    main_done()
                    prefetch()
                for hh in range(4):
                    h_ = 4 * kh + hh
                    sga = next_slab()
                    b = alloc_bank()
                    inproj(sga, 128, 512, b)
                    act(mixT[:, h_, :], ps[b][:, :], AF.Silu, [bank_T[b]], [mix_T[h_]])
                    main_done()
                    prefetch()

                pts = {}

                def s_step(qb, kh=kh, qb_=qb_, pts=pts, j=j):
                    for kbi in range(3):
                        kblk = qb + kbi
                        b = alloc_bank()
                        S.mm_group([mm(ps[b][:, :], kT[:, kh, kblk * 128:(kblk + 1) * 128],
                                       qT[:, qb_, :, qb * 128:(qb + 1) * 128], True, True)],
                                   [kT_T[kh][kblk // 3]] + q_T[qb_], [bank_T[b]])
                        pt, pt_T = tmpB.next(hold=True)
                        act(pt[:, 0:512], ps[b][:, :], AF.Exp, [bank_T[b]], [pt_T], scale=SCALE)
                        if kbi != 1:
                            if kbi == 0:
                                mi = 2 if (j == 0 and qb == 0) else 0
                            else:
                                mi = 3 if (j == NTILE - 1 and qb == 3) else 1
                            tt("dve", pt[:, 0:512].rearrange("p (a b) -> p a b", a=4),
                               pt[:, 0:512].rearrange("p (a b) -> p a b", a=4),
                               masks_bf[:, mi:mi + 1, :].broadcast_to([128, 4, 128]), ALU.mult,
                               [pt_T, ones_T], [pt_T])
                        pts[(qb, kbi)] = (pt, pt_T)

                def pv_step(qb, kh=kh, pts=pts):
                    bo = alloc_bank()
                    bd = alloc_bank()
                    fo, fd = [], []
                    rd = []
                    for kbi in range(3):
                        kblk = qb + kbi
                        pt, pt_T = pts[(qb, kbi)]
                        rd.append(pt_T)
                        fo.append(mm(ps[bo][:, :], v_sb[:, kblk, kh * 128:(kh + 1) * 128], pt[:, 0:512],
                                     kbi == 0, kbi == 2))
                        fd.append(mm(ps[bd][:, :], ones_bf[:, :], pt[:, 0:512], kbi == 0, kbi == 2))
                    S.mm_group(fo, rd + [v_T[kh][0], v_T[kh][1]], [bank_T[bo]])
                    S.mm_group(fd, rd + [ones_T], [bank_T[bd]])
                    for kbi in range(3):
                        tmpB.release(pts[(qb, kbi)][0])
                    den, den_T = tmpF.next()
                    for hh in range(4):
                        h_ = 4 * kh + hh
                        ts("dve", den[:, hh * 128:(hh + 1) * 128], ps[bd][:, hh * 128:(hh + 1) * 128],
                           esink[:, h_:h_ + 1], None, ALU.add, ALU.bypass, [bank_T[bd], esink_T],
                           [den_T] if hh == 0 else (), () if hh == 0 else [den_T])
                    S.op("dve", lambda e, den=den: e.reciprocal(out=den[:, 0:512], in_=den[:, 0:512]),
                         [den_T], [den_T])
                    o, o_T = tmpF.next()
                    tt("dve", o[:, 0:512], ps[bo][:, :], den[:, 0:512], ALU.mult, [bank_T[bo], den_T], [o_T])
                    rows = mix_T[4 * kh:4 * kh + 4]
                    dst = mixT[:, 4 * kh:4 * kh + 4, qb * 128:(qb + 1) * 128]
                    S.op("dve", lambda e, dst=dst, o=o: e.tensor_tensor(
                        out=dst, in0=dst, in1=o[:, 0:512].rearrange("p (a b) -> p a b", a=4), op=ALU.mult),
                        [o_T] + rows, (), rows)

                for t_ in range(5):
                    def step(t_=t_, s_step=s_step, pv_step=pv_step):
                        if t_ < 4:
                            s_step(t_)
                        if t_ >= 1:
                            pv_step(t_ - 1)
                    defer(step, 1 + t_)

            if j + 1 < NTILE:
                xprep_slots(j + 1)[0]()
            if debug and j == 0:
                flush()
                dma("sp", dbg_mix, mixT[:, :, :].rearrange("p a b -> p (a b)"), mix_T, (), "cst")
                dma("sp", dbg_xT, xT[:, :, :].rearrange("p a b -> p (a b)"), xT_T, (), "cst")
                dma("sp", dbg_kT, kT[:, :, :].rearrange("p a b -> p (a b)"),
                    [t for p_ in kT_T for t in p_], (), "cst")
                dma("sp", dbg_v, v_sb[:, :, :].rearrange("p a b -> p (a b)"),
                    [t for p_ in v_T for t in p_], (), "cst")
                dma("sp", dbg_q, qT[:, :, :, :].rearrange("p a b c -> p (a b c)"),
                    [t for p_ in q_T for t in p_], (), "cst")
            nslots = xprep_slots(j + 1)[1] if j + 1 < NTILE else [[] for _ in range(32)]
            slot_i = 0
            for cg in range(8):
                r0 = 128 + TT * j
                dma("pool", xres[:, :, :],
                    x_ext[r0:r0 + TT, cg * 512:(cg + 1) * 512].rearrange("(t p) c -> p t c", p=128),
                    [], [xres_T], "xres")
                obanks = [alloc_bank(hold=True) for _ in range(4)]
                for si, sub in enumerate((2, 3, 0, 1)):
                    if cg == 0 and sub == 0:
                        flush()
                    so = next_slab()
                    for tb in range(4):
                        b = obanks[tb]
                        fns = [mm(ps[b][:, :], mixT[:, sub * 8 + kcl, tb * 128:(tb + 1) * 128],
                                  wring[so][:, kcl * 512:(kcl + 1) * 512], si == 0 and kcl == 0,
                                  si == 3 and kcl == 7) for kcl in range(8)]
                        S.mm_group(fns, [slot_T[so]] + mix_T[sub * 8:sub * 8 + 8],
                                   [bank_T[b]] if si == 0 else (), () if si == 0 else [bank_T[b]])
                    main_done()
                    prefetch()
                    for st in nslots[slot_i]:
                        st()
                    slot_i += 1
                for tb in range(4):
                    b = obanks[tb]
                    tt("dve", xres[:, tb, :], ps[b][:, :], xres[:, tb, :], ALU.add, [bank_T[b], xres_T],
                       (), [xres_T])
                    busy[b] = False
                o0 = TT * j
                dma("pool", out_d[o0:o0 + TT, cg * 512:(cg + 1) * 512].rearrange("(t p) c -> p t c", p=128),
                    xres[:, :, :], [xres_T], [out_T], "ost")
                store_count += 1
        flush()
        S.emit("pool", None, {"ost": S.cnt["ost"], "cst": S.cnt["cst"]})

        sem = S.sem

        def replay(eng_name, e):
            for waits, fn, sig, inc in S.streams[eng_name]:
                for s, v in waits:
                    e.wait_ge(sem[s], v)
                if fn is None:
                    continue
                ins = fn(e)
                if sig is not None:
                    ins.then_inc(sem[sig], inc)

        with nc.Block() as block:
            @block.tensor
            def _(e):
                replay("pe", e)

            @block.scalar
            def _(e):
                replay("act", e)

            @block.vector
            def _(e):
                replay("dve", e)

            @block.gpsimd
            def _(e):
                replay("pool", e)

            @block.sync
            def _(e):
                replay("sp", e)
    return nc


def _layout_weights(w_in, pw_w, w_out):
    w_in = np.asarray(w_in, dtype=np.float32).reshape(32, 128, 88, 128)
    wg = np.ascontiguousarray(w_in.transpose(2, 1, 0, 3)).reshape(88, 128, 4096)
    pw = np.asarray(pw_w, dtype=np.float32).reshape(16, 128, 8, 2, 128)
    pwr = np.ascontiguousarray(pw.transpose(2, 1, 3, 0, 4)).reshape(8, 128, 4096)
    wo = np.asarray(w_out, dtype=np.float32).reshape(4, 8, 128, 8, 512)
    wor = np.ascontiguousarray(wo.transpose(3, 0, 2, 1, 4)).reshape(8, 4, 128, 4096)
    slabs = []
    for i in range(16):
        slabs += [wg[40 + i], wg[56 + i]]
    for g in range(4):
        slabs.append(wg[20 + g])
    for kh in range(4):
        slabs.append(wg[16 + kh])
    for k in range(8):
        slabs += [wg[72 + 2 * k], wg[72 + 2 * k + 1], pwr[k]]
    for kh in range(4):
        for hh in range(4):
            slabs.append(wg[4 * kh + hh])
        for hh in range(4):
            slabs.append(wg[24 + 4 * kh + hh])
    for cg in range(8):
        for sub in (2, 3, 0, 1):
            slabs.append(wor[cg, sub])
    return np.stack(slabs, axis=0)


def _consts(half, norm_gain, q_norm_gain, k_norm_gain, attn_sink, conv_dw_w, conv_dw_b,
            conv_ln_gain, conv_ln_bias, conv_pw_b):
    c = np.zeros((128, NCST), np.float32)
    c[:, C_ID:C_ID + 128] = np.eye(128, dtype=np.float32)
    kk = np.arange(128)[:, None]
    qq = np.arange(128)[None, :]
    m_prev = (kk >= qq).astype(np.float32)
    m_next = (kk <= qq).astype(np.float32)
    c[:, C_MASK + 0:C_MASK + 128] = m_prev
    c[:, C_MASK + 128:C_MASK + 256] = m_next
    c[:, C_MASK + 256:C_MASK + 384] = m_prev if half == 1 else 0.0
    c[:, C_MASK + 384:C_MASK + 512] = m_next if half == 0 else 0.0
    sw = np.zeros((32, 32), np.float32)
    for m in range(32):
        sw[(m + 16) % 32, m] = 1.0
    c[0:32, C_SW:C_SW + 32] = sw
    c[:, C_GAIN:C_GAIN + 32] = np.asarray(norm_gain, np.float32).reshape(32, 128).T
    c[:, C_QKG] = np.asarray(q_norm_gain, np.float32).reshape(128)
    c[:, C_QKG + 1] = np.asarray(k_norm_gain, np.float32).reshape(128)
    c[:, C_SINK:C_SINK + 16] = np.asarray(attn_sink, np.float32).reshape(1, 16)
    dw = np.asarray(conv_dw_w, np.float32).reshape(31, 16, 128)
    c[:, C_DW:C_DW + 496] = dw.transpose(2, 1, 0).reshape(128, 496)
    c[:, C_DWB:C_DWB + 16] = np.asarray(conv_dw_b, np.float32).reshape(16, 128).T
    c[:, C_LNG:C_LNG + 16] = np.asarray(conv_ln_gain, np.float32).reshape(16, 128).T
    c[:, C_LNB:C_LNB + 16] = np.asarray(conv_ln_bias, np.float32).reshape(16, 128).T
    c[:, C_PWB:C_PWB + 16] = np.asarray(conv_pw_b, np.float32).reshape(16, 128).T
    c[:, C_EPS] = NORM_EPS
    c[:, C_EPS + 1] = LN_EPS
    return c


def _rope_table(half):
    pos = (half * TOK - 128 + np.arange(EXT)).astype(np.float32)
    inv = np.power(np.float32(ROPE_THETA), -np.arange(16, dtype=np.float32) * np.float32(2.0 / 32.0))
    ang = pos[None, :] * inv[:, None]
    cos = np.cos(ang).astype(np.float32)
    sin = np.sin(ang).astype(np.float32)
    t = np.zeros((32, 2, EXT), np.float32)
    t[0:16, 0] = cos
    t[16:32, 0] = cos
    t[0:16, 1] = -sin
    t[16:32, 1] = sin
    return t


_NC_CACHE = {}


def kernel(x, norm_gain, w_in, q_norm_gain, k_norm_gain, attn_sink, conv_dw_w, conv_dw_b,
           conv_ln_gain, conv_ln_bias, conv_pw_w, conv_pw_b, w_out):
    x = np.asarray(x, dtype=np.float32)
    wall = _layout_weights(np.asarray(w_in)[0], np.asarray(conv_pw_w)[0], np.asarray(w_out)[0])
    ropes = [_rope_table(0), _rope_table(1)]
    csts = [_consts(h, np.asarray(norm_gain)[0], np.asarray(q_norm_gain)[0], np.asarray(k_norm_gain)[0],
                    np.asarray(attn_sink)[0], np.asarray(conv_dw_w)[0], np.asarray(conv_dw_b)[0],
                    np.asarray(conv_ln_gain)[0], np.asarray(conv_ln_bias)[0], np.asarray(conv_pw_b)[0])
            for h in (0, 1)]
    in_maps = []
    for c in range(NCORES):
        b, half = c // 2, c % 2
        xe = np.zeros((EXT, D_MODEL), np.float32)
        lo = half * TOK - 128
        s0, s1 = max(lo, 0), min(lo + EXT, SEQ)
        xe[s0 - lo:s1 - lo] = x[b, s0:s1]
        in_maps.append({"x_ext": xe, "wall": wall, "cst": csts[half], "rope": ropes[half]})
    if "nc" not in _NC_CACHE:
        _NC_CACHE["nc"] = build_program()
    res = run_bass_kernel_spmd(_NC_CACHE["nc"], in_maps, core_ids=list(range(NCORES)))
    out = np.empty((BATCH, SEQ, D_MODEL), np.float32)
    for c in range(NCORES):
        b, half = c // 2, c % 2
        out[b, half * TOK:(half + 1) * TOK] = np.asarray(res.results[c]["out"], dtype=np.float32)
    return out
```

```python
import math
from collections import deque
from contextlib import ExitStack

import numpy as np
import concourse.bass as bass
import concourse.mybir as mybir
from concourse.bass_utils import run_bass_kernel_spmd

F32 = mybir.dt.float32
BF16 = mybir.dt.bfloat16
AF = mybir.ActivationFunctionType
ALU = mybir.AluOpType

D_MODEL = 4096
SEQ = 4096
BATCH = 4
NCORES = 8
TOK = 2048
EXT = TOK + 256
TT = 512
NTILE = TOK // TT
TW = TT + 256
NSLOT = 4
NSLAB_W = 128
CAST_BOUNDS = list(range(0, 16, 2)) + list(range(16, 129, 8))
NCASTG = len(CAST_BOUNDS) - 1


def _cast_group(n):
    for g in range(NCASTG):
        if CAST_BOUNDS[g] <= n < CAST_BOUNDS[g + 1]:
            return g
    raise ValueError(n)
NORM_EPS = 1e-6
LN_EPS = 1e-5
ROPE_THETA = 500000.0
SCALE = 1.0 / math.sqrt(128.0)

C_ID = 0
C_MASK = 128
C_SW = 640
C_GAIN = 672
C_QKG = 704
C_SINK = 706
C_DW = 722
C_DWB = 1218
C_LNG = 1234
C_LNB = 1250
C_PWB = 1266
C_EPS = 1282
NCST = 1284


class T:
    __slots__ = ("w", "r", "war")

    def __init__(self):
        self.w = {}
        self.r = {}
        self.war = {}


def _merge(d, s):
    for k, v in s.items():
        if d.get(k, 0) < v:
            d[k] = v


class Sched:
    def __init__(self):
        self.streams = {e: [] for e in ("pe", "act", "dve", "pool", "sp")}
        self.cnt = {}
        self.waited = {e: {} for e in self.streams}
        self.sem = {}

    def new_sem(self, name):
        self.cnt[name] = 0

    def emit(self, eng, fn, deps, sig=None, inc=1):
        waits = []
        wd = self.waited[eng]
        for s, v in deps.items():
            if eng == "pe" and s == "pe":
                continue
            if wd.get(s, 0) >= v:
                continue
            wd[s] = v
            waits.append((s, v))
        h = {}
        if sig is not None:
            self.cnt[sig] += inc
            h = {sig: self.cnt[sig]}
        self.streams[eng].append((waits, fn, sig, inc))
        return h

    def op(self, eng, fn, reads=(), writes=(), partial=(), sig="auto", inc=1):
        deps = {}
        for t in reads:
            _merge(deps, t.w)
        for t in writes:
            war = dict(t.w)
            _merge(war, t.r)
            t.war = war
            t.w = {}
            t.r = {}
            _merge(deps, war)
        for t in partial:
            _merge(deps, t.war)
        if sig == "auto":
            sig = eng
        h = self.emit(eng, fn, deps, sig, inc)
        for t in reads:
            _merge(t.r, h)
        for t in writes:
            _merge(t.w, h)
        for t in partial:
            _merge(t.w, h)
        return h

    def mm_group(self, fns, reads=(), writes=(), partial=()):
        deps = {}
        for t in reads:
            _merge(deps, t.w)
        for t in writes:
            war = dict(t.w)
            _merge(war, t.r)
            t.war = war
            t.w = {}
            t.r = {}
            _merge(deps, war)
        for t in partial:
            _merge(deps, t.war)
        n = len(fns)
        h = {}
        for i, fn in enumerate(fns):
            h = self.emit("pe", fn, deps if i == 0 else {}, "pe" if i == n - 1 else None)
        for t in reads:
            _merge(t.r, h)
        for t in writes:
            _merge(t.w, h)
        for t in partial:
            _merge(t.w, h)
        return h


class Ring:
    def __init__(self, aps):
        self.aps = aps
        self.ts = [T() for _ in aps]
        self.held = [0] * len(aps)
        self.i = 0

    def next(self, hold=False):
        for _ in range(2 * len(self.aps)):
            i = self.i
            self.i = (i + 1) % len(self.aps)
            if not self.held[i]:
                if hold:
                    self.held[i] = 1
                return self.aps[i], self.ts[i]
        raise RuntimeError("tmp ring exhausted")

    def release(self, ap):
        for i, a in enumerate(self.aps):
            if a is ap:
                self.held[i] = 0
                return
        raise RuntimeError("release of unknown buffer")


def build_program(debug=False):
    nc = bass.Bass("TRN2", target_bir_lowering=False)
    x_ext = nc.dram_tensor("x_ext", [EXT, D_MODEL], F32, kind="ExternalInput").ap()
    wall = nc.dram_tensor("wall", [NSLAB_W, 128, 4096], F32, kind="ExternalInput").ap()
    cst_d = nc.dram_tensor("cst", [128, NCST], F32, kind="ExternalInput").ap()
    rope_d = nc.dram_tensor("rope", [32, 2, EXT], F32, kind="ExternalInput").ap()
    out_d = nc.dram_tensor("out", [TOK, D_MODEL], F32, kind="ExternalOutput").ap()
    wbf = nc.dram_tensor("wbf", [NSLAB_W, 128, 4096], BF16, kind="Internal").ap()
    dgd = nc.dram_tensor("dgd", [16, 128, 31 * 128], BF16, kind="Internal").ap()
    if debug:
        dbg_mix = nc.dram_tensor("dbg_mix", [128, 32 * TT], BF16, kind="ExternalOutput").ap()
        dbg_xT = nc.dram_tensor("dbg_xT", [128, 32 * TW], BF16, kind="ExternalOutput").ap()
        dbg_kT = nc.dram_tensor("dbg_kT", [128, 4 * TW], BF16, kind="ExternalOutput").ap()
        dbg_v = nc.dram_tensor("dbg_v", [128, 6 * 512], BF16, kind="ExternalOutput").ap()
        dbg_q = nc.dram_tensor("dbg_q", [128, 2 * 4 * TT], BF16, kind="ExternalOutput").ap()

    S = Sched()
    es = ExitStack()
    with es:
        def sb(name, shape, dt):
            return es.enter_context(nc.sbuf_tensor(name, shape, dt))

        xT = sb("xT", [128, 32, TW], BF16)
        mixT = sb("mixT", [128, 32, TT], BF16)
        kT = sb("kT", [128, 4, TW], BF16)
        v_sb = sb("v_sb", [128, 6, 512], BF16)
        qT = sb("qT", [128, 2, 4, TT], BF16)
        xb = sb("xb", [128, D_MODEL], F32)
        xs = sb("xs", [128, 2, 512], F32)
        wring = [sb(f"wring{i}", [128, 4096], BF16) for i in range(NSLOT)]
        rope_t = sb("rope_t", [32, 2, TW], F32)
        NTF, NTB = 8, 12
        tmpF = Ring([sb(f"tmpF{i}", [128, 544], F32) for i in range(NTF)])
        tmpB = Ring([sb(f"tmpB{i}", [128, 544], BF16) for i in range(NTB)])
        xres = sb("xres", [128, 4, 512], F32)
        cst = sb("cst_sb", [128, NCST], F32)
        ident_bf = sb("ident_bf", [128, 128], BF16)
        ones_bf = sb("ones_bf", [128, 128], BF16)
        sw_bf = sb("sw_bf", [32, 32], BF16)
        masks_bf = sb("masks_bf", [128, 4, 128], BF16)
        esink = sb("esink", [128, 16], F32)
        small = sb("small", [128, 16], F32)
        ps = [es.enter_context(nc.psum_tensor(f"ps{i}", [128, 512], F32)) for i in range(8)]

        sem_names = ["pe", "act", "dve", "pool", "cst", "xb", "rope", "xres", "ost", "dg"]
        sem_names += [f"w{i}" for i in range(NSLOT)]
        sem_names += [f"cast{i}" for i in range(NCASTG)]
        for n in sem_names:
            S.sem[n] = es.enter_context(nc.semaphore(n))
            S.new_sem(n)

        xT_T = [T() for _ in range(6)]
        mix_T = [T() for _ in range(32)]
        kT_T = [[T(), T()] for _ in range(4)]
        v_T = [[T(), T()] for _ in range(4)]
        q_T = [[T() for _ in range(4)] for _ in range(2)]
        xb_T, rope_T, xres_T = T(), T(), T()
        xs_T = [T(), T()]
        slot_T = [T() for _ in range(NSLOT)]
        cst_T, esink_T = T(), T()
        small_T = [T() for _ in range(16)]
        bank_T = [T() for _ in range(8)]
        cast_T = [T() for _ in range(NCASTG)]
        dgd_T = [T() for _ in range(16)]
        out_T = T()
        busy = [False] * 8
        rr = [0]

        def alloc_bank(hold=False):
            for _ in range(16):
                i = rr[0]
                rr[0] = (i + 1) % 8
                if not busy[i]:
                    if hold:
                        busy[i] = True
                    return i
            raise RuntimeError("no free PSUM bank")

        def act(out, in_, func, reads, writes=(), partial=(), scale=1.0, bias=None, accum_out=None):
            kw = {}
            if bias is not None:
                kw["bias"] = bias
            if accum_out is not None:
                kw["accum_out"] = accum_out
            return S.op("act", lambda e: e.activation(out=out, in_=in_, func=func, scale=scale, **kw),
                        reads, writes, partial)

        def tt(eng, out, in0, in1, op, reads, writes=(), partial=()):
            return S.op(eng, lambda e: e.tensor_tensor(out=out, in0=in0, in1=in1, op=op),
                        reads, writes, partial)

        def ts(eng, out, in0, s1, s2, op0, op1, reads, writes=(), partial=()):
            return S.op(eng, lambda e: e.tensor_scalar(out=out, in0=in0, scalar1=s1, scalar2=s2, op0=op0, op1=op1),
                        reads, writes, partial)

        def stt(out, in0, scalar, in1, op0, op1, reads, writes=(), partial=()):
            return S.op("dve", lambda e: e.scalar_tensor_tensor(out=out, in0=in0, scalar=scalar, in1=in1,
                                                               op0=op0, op1=op1),
                        reads, writes, partial)

        def dma(q, out, in_, reads, writes, sem):
            return S.op(q, lambda e: e.dma_start(out=out, in_=in_), reads, writes, (), sig=sem, inc=16)

        def mm(out, lhsT, rhs, start, stop):
            return lambda e: e.matmul(out, lhsT=lhsT, rhs=rhs, start=start, stop=stop)

        pending = deque()
        mcount = [0]

        def defer(fn, lag):
            rel = mcount[0] + lag
            idx = len(pending)
            for k, (r, _) in enumerate(pending):
                if r > rel:
                    idx = k
                    break
            pending.insert(idx, (rel, fn))

        def flush(upto=None):
            while pending and (upto is None or pending[0][0] <= upto):
                _, fn = pending.popleft()
                fn()

        def main_done():
            mcount[0] += 1
            flush(mcount[0])

        tile_stream = []
        sid = 0
        for i in range(16):
            tile_stream += [("w", sid), ("w", sid + 1), ("dg", i)]
            sid += 2
        for _ in range(8):
            tile_stream.append(("w", sid))
            sid += 1
        for _ in range(24):
            tile_stream.append(("w", sid))
            sid += 1
        for _ in range(32):
            tile_stream.append(("w", sid))
            sid += 1
        for _ in range(32):
            tile_stream.append(("w", sid))
            sid += 1
        assert sid == NSLAB_W
        full_stream = tile_stream * NTILE
        ld = {"next_load": 0, "next_use": 0}
        slot_hold = [0] * NSLOT
        wb_needed = {}

        def issue_load(force=False):
            while True:
                n = ld["next_load"]
                if n >= len(full_stream):
                    return False
                if not slot_hold[n % NSLOT]:
                    break
                if not force:
                    return False
                assert pending, "held slot without pending consumer"
                _, fn = pending.popleft()
                fn()
                return True
            ld["next_load"] = n + 1
            kind, idx = full_stream[n]
            slot = n % NSLOT
            tile_i = n // len(tile_stream)
            if kind == "w" and (tile_i == 0 or (tile_i == 1 and idx % 2 == 1)):
                g = _cast_group(idx)
                dma("pool", wring[slot][:, :], wall[idx], [], [slot_T[slot]], f"w{slot}")
                if tile_i == idx % 2:
                    wb_needed[n] = (idx, g)
            elif kind == "w":
                dma("sp", wring[slot][:, :], wbf[idx], [cast_T[_cast_group(idx)]], [slot_T[slot]], f"w{slot}")
            else:
                dma("sp", wring[slot][:, 0:31 * 128], dgd[idx], [dgd_T[idx]], [slot_T[slot]], f"w{slot}")
            return True

        def next_slab():
            n = ld["next_use"]
            ld["next_use"] = n + 1
            while ld["next_load"] <= n:
                issue_load(force=True)
            if n in wb_needed:
                idx, g = wb_needed.pop(n)
                hh = dma("sp", wbf[idx], wring[n % NSLOT][:, :], [slot_T[n % NSLOT]], (), f"cast{g}")
                _merge(cast_T[g].w, hh)
            return n % NSLOT

        def prefetch():
            while ld["next_load"] < min(len(full_stream), ld["next_use"] + NSLOT - 1):
                if not issue_load():
                    break

        dma("sp", cst[:, :], cst_d, [], [cst_T], "cst")
        ones_T = T()
        S.op("dve", lambda e: e.memset(ones_bf[:, :], 1.0), [], [ones_T])
        S.op("dve", lambda e: e.tensor_copy(out=ident_bf[:, :], in_=cst[:, C_ID:C_ID + 128]), [cst_T], [ones_T])
        S.op("dve", lambda e: e.tensor_copy(out=sw_bf[:, :], in_=cst[0:32, C_SW:C_SW + 32]), [cst_T], [ones_T])
        S.op("dve", lambda e: e.tensor_copy(out=masks_bf[:, :, :],
                                            in_=cst[:, C_MASK:C_MASK + 512].rearrange("p (a b) -> p a b", a=4)),
             [cst_T], [ones_T])
        act(esink[:, :], cst[:, C_SINK:C_SINK + 16], AF.Exp, [cst_T], [esink_T])

        def build_diag(i):
            if i < 2:
                stage = mixT[:, 8 * i:8 * i + 8, :].rearrange("p a b -> p (a b)")
                st_T = mix_T[8 * i:8 * i + 8]
            else:
                stage = xres[:, :, :].rearrange("p a b -> p (a b)").bitcast(BF16)
                st_T = [xres_T]
            for j in range(31):
                c = C_DW + i * 31 + j
                S.op("dve", lambda e, j=j, c=c, stage=stage: e.tensor_scalar(
                    out=stage[:, j * 128:(j + 1) * 128], in0=ident_bf[:, :],
                    scalar1=cst[:, c:c + 1], scalar2=0.0, op0=ALU.mult, op1=ALU.add),
                    [cst_T, ones_T], st_T if j == 0 else (), () if j == 0 else st_T)
            dma("sp", dgd[i], stage[:, 0:31 * 128], st_T, [dgd_T[i]], "dg")

        eps_n = cst[:, C_EPS:C_EPS + 1]
        eps_l = cst[:, C_EPS + 1:C_EPS + 2]

        xb2 = mixT[:, 16:32, :].rearrange("p a b -> p (a b)").bitcast(F32)
        xb2_T = mix_T[16:32]

        def xprep_plan(j):
            loads, works = [], []
            for blk in range(6):
                if j == 0 and blk % 2 == 1:
                    buf, buf_T, so = xb2, xb2_T, 8
                else:
                    buf, buf_T, so = xb[:, :], [xb_T], 0

                def load(blk=blk, buf=buf, buf_T=buf_T, so=so):
                    r0 = TT * j + 128 * blk
                    dma("sp", buf, x_ext[r0:r0 + 128, :], [], buf_T, "xb")
                    junk = xs[:, :, :].rearrange("p a b -> p (a b)").bitcast(BF16)[:, 0:2048]
                    act(junk, buf[:, 0:2048], AF.Square, buf_T, [xs_T[0], xs_T[1], small_T[so + 0]],
                        accum_out=small[:, so + 0:so + 1])
                    act(junk, buf[:, 2048:4096], AF.Square, buf_T, [xs_T[0], xs_T[1], small_T[so + 1]],
                        accum_out=small[:, so + 1:so + 2])
                    tt("dve", small[:, so + 2:so + 3], small[:, so + 0:so + 1], small[:, so + 1:so + 2], ALU.add,
                       [small_T[so + 0], small_T[so + 1]], [small_T[so + 2]])
                    act(small[:, so + 3:so + 4], small[:, so + 2:so + 3], AF.Ln, [small_T[so + 2], cst_T],
                        [small_T[so + 3]], scale=1.0 / D_MODEL, bias=eps_n)
                    act(small[:, so + 4:so + 5], small[:, so + 3:so + 4], AF.Exp, [small_T[so + 3]],
                        [small_T[so + 4]], scale=-0.5)
                loads.append(load)
                ws = []
                for c in range(8):
                    def work(blk=blk, c=c, buf=buf, buf_T=buf_T, so=so):
                        h = c % 2
                        act(xs[:, h, :], buf[:, c * 512:(c + 1) * 512], AF.Copy, list(buf_T) + [small_T[so + 4]],
                            [xs_T[h]], scale=small[:, so + 4:so + 5])
                        b = alloc_bank()
                        fns = []
                        for kk in range(4):
                            fns.append(lambda e, kk=kk, b=b, h=h: e.transpose(
                                ps[b][:, kk * 128:(kk + 1) * 128], xs[:, h, kk * 128:(kk + 1) * 128],
                                cst[:, C_ID:C_ID + 128]))
                        S.mm_group(fns, [xs_T[h], cst_T], [bank_T[b]])
                        kc0 = c * 4
                        first = (c == 0)
                        tt("dve", xT[:, kc0:kc0 + 4, blk * 128:(blk + 1) * 128],
                           ps[b][:, :].rearrange("p (a b) -> p a b", a=4),
                           cst[:, C_GAIN + kc0:C_GAIN + kc0 + 4].unsqueeze(2).broadcast_to([128, 4, 128]),
                           ALU.mult, [bank_T[b], cst_T],
                           [xT_T[blk]] if first else (), () if first else [xT_T[blk]])
                    ws.append(work)
                works.append(ws)
            return loads, works

        def xprep_slots(j):
            loads, works = xprep_plan(j)
            slots = [[] for _ in range(32)]
            for k in range(2, 6):
                for i in range(4):
                    slots[5 * (k - 2) + i] += [works[k][2 * i], works[k][2 * i + 1]]
                if k + 1 < 6:
                    slots[5 * (k - 2) + 3].append(loads[k + 1])
            for pc in range(4):
                def halo_copy(pc=pc):
                    act(xT[:, 8 * pc:8 * pc + 8, 0:256], xT[:, 8 * pc:8 * pc + 8, 512:768], AF.Copy,
                        [xT_T[4], xT_T[5]], [xT_T[0], xT_T[1]] if pc == 0 else (),
                        () if pc == 0 else [xT_T[0], xT_T[1]])
                slots[pc].insert(0, halo_copy)
            return loads[2], slots

        def xT_reads(t0, n):
            return [xT_T[b] for b in range(t0 // 128, (t0 + n - 1) // 128 + 1)]

        def inproj(slot, t0, n, b, col0=0):
            fns = []
            for kc in range(32):
                fns.append(mm(ps[b][:, col0:col0 + n], wring[slot][:, kc * 128:(kc + 1) * 128],
                              xT[:, kc, t0:t0 + n], kc == 0, kc == 31))
            S.mm_group(fns, [slot_T[slot]] + xT_reads(t0, n), [bank_T[b]])

        def norm_rope(b, n, gcol, tok0, dest, dest_T, lag3=1):
            sq, sq_T = tmpB.next(hold=True)
            qg, qg_T = tmpF.next(hold=True)
            act(sq[:, 0:n], ps[b][:, 0:n], AF.Square, [bank_T[b]], [sq_T])
            act(qg[:, 0:n], ps[b][:, 0:n], AF.Copy, [bank_T[b], cst_T], [qg_T],
                scale=cst[:, C_QKG + gcol:C_QKG + gcol + 1])

            def stage2():
                b2 = alloc_bank()
                S.mm_group([mm(ps[b2][:, 0:n], ones_bf[:, :], sq[:, 0:n], True, True)],
                           [ones_T, sq_T], [bank_T[b2]])
                r, r_T = tmpF.next()
                act(r[:, 0:n], ps[b2][:, 0:n], AF.Ln, [bank_T[b2], cst_T], [r_T], scale=1.0 / 128.0, bias=eps_n)
                act(r[:, 0:n], r[:, 0:n], AF.Exp, [r_T], [r_T], scale=-0.5)
                tt("dve", dest, qg[:, 0:n], r[:, 0:n], ALU.mult, [qg_T, r_T], [dest_T])
                tmpB.release(sq)
                tmpF.release(qg)

                def stage3():
                    b3 = alloc_bank()
                    S.mm_group([mm(ps[b3][0:32, 0:n], sw_bf[:, :], dest[0:32], True, True)],
                               [ones_T, dest_T], [bank_T[b3]])
                    t1, t1_T = tmpF.next()
                    t2, t2_T = tmpF.next()
                    tt("dve", t1[0:32, 0:n], dest[0:32], rope_t[:, 0, tok0:tok0 + n], ALU.mult,
                       [dest_T, rope_T], [t1_T])
                    tt("dve", t2[0:32, 0:n], ps[b3][0:32, 0:n], rope_t[:, 1, tok0:tok0 + n], ALU.mult,
                       [bank_T[b3], rope_T], [t2_T])
                    tt("dve", dest[0:32], t1[0:32, 0:n], t2[0:32, 0:n], ALU.add, [t1_T, t2_T, dest_T], [dest_T])
                defer(stage3, lag3)
            defer(stage2, 1)

        loads0, works0 = xprep_plan(0)
        loads0[0]()
        loads0[1]()
        build_diag(0)
        build_diag(1)
        for k in range(6):
            for i in range(4):
                works0[k][2 * i]()
                works0[k][2 * i + 1]()
            if k + 2 < 6:
                loads0[k + 2]()
        store_count = 0
        for j in range(NTILE):
            dma("sp", rope_t[:, :, :], rope_d[:, :, TT * j:TT * j + TW], [], [rope_T], "rope")

            for i in range(16):
                if j == 0 and i + 2 < 16:
                    build_diag(i + 2)
                sa = next_slab()
                a_sb, a_T = tmpF.next(hold=True)
                for s in range(2):
                    b = alloc_bank()
                    inproj(sa, 112 + 272 * s, 272, b)
                    act(a_sb[:, 272 * s:272 * s + 272], ps[b][:, 0:272], AF.Copy, [bank_T[b]],
                        [a_T] if s == 0 else (), () if s == 0 else [a_T])
                main_done()
                prefetch()
                sbb = next_slab()
                sg, sg_T = tmpF.next(hold=True)
                for s in range(2):
                    b = alloc_bank()
                    inproj(sbb, 112 + 272 * s, 272, b)
                    act(sg[:, 272 * s:272 * s + 272], ps[b][:, 0:272], AF.Sigmoid, [bank_T[b]],
                        [sg_T] if s == 0 else (), () if s == 0 else [sg_T])
                h_sb, h_T = tmpB.next(hold=True)
                tt("dve", h_sb[:, :], a_sb[:, :], sg[:, :], ALU.mult, [a_T, sg_T], [h_T])
                tmpF.release(a_sb)
                tmpF.release(sg)
                main_done()
                sd = next_slab()
                slot_hold[sd] += 1

                def conv(i=i, sd=sd, h_sb=h_sb, h_T=h_T):
                    b = alloc_bank()
                    fns = [mm(ps[b][:, :], wring[sd][:, jj * 128:(jj + 1) * 128], h_sb[:, 1 + jj:1 + jj + 512],
                              jj == 0, jj == 30) for jj in range(31)]
                    S.mm_group(fns, [slot_T[sd], h_T], [bank_T[b]])
                    act(mixT[:, i, :], ps[b][:, :], AF.Identity, [bank_T[b], cst_T], [mix_T[i]],
                        bias=cst[:, C_DWB + i:C_DWB + i + 1])
                    slot_hold[sd] -= 1
                    tmpB.release(h_sb)
                    prefetch()
                defer(conv, 1)

            def ln_stats():
                bs = alloc_bank(hold=True)
                bq = alloc_bank(hold=True)
                S.mm_group([mm(ps[bs][:, :], ones_bf[:, :], mixT[:, i, :], i == 0, i == 15) for i in range(16)],
                           [ones_T] + mix_T[0:16], [bank_T[bs]])
                for i in range(16):
                    sq, sq_T = tmpB.next()
                    act(sq[:, 0:512], mixT[:, i, :], AF.Square, [mix_T[i]], [sq_T])
                    S.mm_group([mm(ps[bq][:, :], ones_bf[:, :], sq[:, 0:512], i == 0, i == 15)],
                               [ones_T, sq_T], [bank_T[bq]] if i == 0 else (), () if i == 0 else [bank_T[bq]])
                mean, mean_T = tmpF.next(hold=True)
                var, var_T = tmpF.next(hold=True)
                rs, rs_T = tmpF.next(hold=True)
                n = 512
                act(mean[:, 0:n], ps[bs][:, :], AF.Copy, [bank_T[bs]], [mean_T], scale=1.0 / 2048.0)
                tt("dve", var[:, 0:n], mean[:, 0:n], mean[:, 0:n], ALU.mult, [mean_T], [var_T])
                stt(var[:, 0:n], ps[bq][:, :], 1.0 / 2048.0, var[:, 0:n], ALU.mult, ALU.subtract,
                    [bank_T[bq], var_T], [var_T])
                busy[bs] = False
                busy[bq] = False
                act(rs[:, 0:n], var[:, 0:n], AF.Ln, [var_T, cst_T], [rs_T], bias=eps_l)
                act(rs[:, 0:n], rs[:, 0:n], AF.Exp, [rs_T], [rs_T], scale=-0.5)
                stt(mean[:, 0:n], mean[:, 0:n], -1.0, rs[:, 0:n], ALU.mult, ALU.mult, [mean_T, rs_T], [mean_T])
                for i in range(16):
                    z, z_T = tmpF.next()
                    tt("dve", z[:, 0:n], mixT[:, i, :], rs[:, 0:n], ALU.mult, [mix_T[i], rs_T], [z_T])
                    tt("dve", z[:, 0:n], z[:, 0:n], mean[:, 0:n], ALU.add, [z_T, mean_T], [z_T])
                    act(mixT[:, i, :], z[:, 0:n], AF.Silu, [z_T, cst_T], [mix_T[i]],
                        scale=cst[:, C_LNG + i:C_LNG + i + 1], bias=cst[:, C_LNB + i:C_LNB + i + 1])
                tmpF.release(mean)
                tmpF.release(var)
                tmpF.release(rs)
            defer(ln_stats, 2)

            for g in range(4):
                sv = next_slab()
                for half, blks in ((0, (0, 1, 2, 3)), (1, (4, 5))):
                    b = alloc_bank()
                    fns = []
                    for bi, blk in enumerate(blks):
                        for kc in range(32):
                            fns.append(mm(ps[b][:, bi * 128:(bi + 1) * 128], xT[:, kc, blk * 128:(blk + 1) * 128],
                                          wring[sv][:, kc * 128:(kc + 1) * 128], kc == 0, kc == 31))
                    S.mm_group(fns, [slot_T[sv]] + [xT_T[blk] for blk in blks], [bank_T[b]])
                    nb = len(blks)
                    S.op("dve", lambda e, b=b, nb=nb, blks=blks, g=g: e.tensor_copy(
                        out=v_sb[:, blks[0]:blks[0] + nb, g * 128:(g + 1) * 128],
                        in_=ps[b][:, 0:nb * 128].rearrange("p (a b) -> p a b", a=nb)),
                        [bank_T[b]], [v_T[g][half]])
                main_done()
                prefetch()
            for kh in range(4):
                sk = next_slab()
                for s in range(2):
                    b = alloc_bank()
                    inproj(sk, 384 * s, 384, b)
                    norm_rope(b, 384, 1, 384 * s, kT[:, kh, 384 * s:384 * s + 384], kT_T[kh][s], lag3=3)
                main_done()
                prefetch()

            for k in range(8):
                gates = []
                for el in range(2):
                    sgc_slot = next_slab()
                    b = alloc_bank()
                    inproj(sgc_slot, 128, 512, b)
                    sgc, sgc_T = tmpB.next(hold=True)
                    act(sgc[:, 0:512], ps[b][:, :], AF.Silu, [bank_T[b]], [sgc_T])
                    gates.append((sgc, sgc_T))
                    main_done()
                    prefetch()
                spw = next_slab()
                slot_hold[spw] += 2
                for el in range(2):
                    e_ = 2 * k + el

                    def pw(e_=e_, el=el, spw=spw, sgc=gates[el][0], sgc_T=gates[el][1]):
                        b2 = alloc_bank()
                        fns = [mm(ps[b2][:, :], wring[spw][:, (el * 16 + cc) * 128:(el * 16 + cc + 1) * 128],
                                  mixT[:, cc, :], cc == 0, cc == 15) for cc in range(16)]
                        S.mm_group(fns, [slot_T[spw]] + mix_T[0:16], [bank_T[b2]])
                        stt(mixT[:, 16 + e_, :], ps[b2][:, :], cst[:, C_PWB + e_:C_PWB + e_ + 1], sgc[:, 0:512],
                            ALU.add, ALU.mult, [bank_T[b2], cst_T, sgc_T], [mix_T[16 + e_]])
                        slot_hold[spw] -= 1
                        tmpB.release(sgc)
                        prefetch()
                    defer(pw, 1)

            for kh in range(4):
                qb_ = kh % 2
                for hh in range(4):
                    sq_slot = next_slab()
                    b = alloc_bank()
                    inproj(sq_slot, 128, 512, b)
                    norm_rope(b, 512, 0, 128, qT[:, qb_, hh, :], q_T[qb_][hh])
                    main_done()
                    prefetch()
                for hh in range(4):
                    h_ = 4 * kh + hh
                    sga = next_slab()
                    b = alloc_bank()
                    inproj(sga, 128, 512, b)
                    act(mixT[:, h_, :], ps[b][:, :], AF.Silu, [bank_T[b]], [mix_T[h_]])
                    main_done()
                    prefetch()

                pts = {}

                def s_step(qb, kh=kh, qb_=qb_, pts=pts, j=j):
                    for kbi in range(3):
                        kblk = qb + kbi
                        b = alloc_bank()
                        S.mm_group([mm(ps[b][:, :], kT[:, kh, kblk * 128:(kblk + 1) * 128],
                                       qT[:, qb_, :, qb * 128:(qb + 1) * 128], True, True)],
                                   [kT_T[kh][kblk // 3]] + q_T[qb_], [bank_T[b]])
                        pt, pt_T = tmpB.next(hold=True)
                        act(pt[:, 0:512], ps[b][:, :], AF.Exp, [bank_T[b]], [pt_T], scale=SCALE)
                        if kbi != 1:
                            if kbi == 0:
                                mi = 2 if (j == 0 and qb == 0) else 0
                            else:
                                mi = 3 if (j == NTILE - 1 and qb == 3) else 1
                            tt("dve", pt[:, 0:512].rearrange("p (a b) -> p a b", a=4),
                               pt[:, 0:512].rearrange("p (a b) -> p a b", a=4),
                               masks_bf[:, mi:mi + 1, :].broadcast_to([128, 4, 128]), ALU.mult,
                               [pt_T, ones_T], [pt_T])
                        pts[(qb, kbi)] = (pt, pt_T)

                def pv_step(qb, kh=kh, pts=pts):
                    bo = alloc_bank()
                    bd = alloc_bank()
                    fo, fd = [], []
                    rd = []
                    for kbi in range(3):
                        kblk = qb + kbi
                        pt, pt_T = pts[(qb, kbi)]
                        rd.append(pt_T)
                        fo.append(mm(ps[bo][:, :], v_sb[:, kblk, kh * 128:(kh + 1) * 128], pt[:, 0:512],
                                     kbi == 0, kbi == 2))
                        fd.append(mm(ps[bd][:, :], ones_bf[:, :], pt[:, 0:512], kbi == 0, kbi == 2))
                    S.mm_group(fo, rd + [v_T[kh][0], v_T[kh][1]], [bank_T[bo]])
                    S.mm_group(fd, rd + [ones_T], [bank_T[bd]])
                    for kbi in range(3):
                        tmpB.release(pts[(qb, kbi)][0])
                    den, den_T = tmpF.next()
                    for hh in range(4):
                        h_ = 4 * kh + hh
                        ts("dve", den[:, hh * 128:(hh + 1) * 128], ps[bd][:, hh * 128:(hh + 1) * 128],
                           esink[:, h_:h_ + 1], None, ALU.add, ALU.bypass, [bank_T[bd], esink_T],
                           [den_T] if hh == 0 else (), () if hh == 0 else [den_T])
                    S.op("dve", lambda e, den=den: e.reciprocal(out=den[:, 0:512], in_=den[:, 0:512]),
                         [den_T], [den_T])
                    o, o_T = tmpF.next()
                    tt("dve", o[:, 0:512], ps[bo][:, :], den[:, 0:512], ALU.mult, [bank_T[bo], den_T], [o_T])
                    rows = mix_T[4 * kh:4 * kh + 4]
                    dst = mixT[:, 4 * kh:4 * kh + 4, qb * 128:(qb + 1) * 128]
                    S.op("dve", lambda e, dst=dst, o=o: e.tensor_tensor(
                        out=dst, in0=dst, in1=o[:, 0:512].rearrange("p (a b) -> p a b", a=4), op=ALU.mult),
                        [o_T] + rows, (), rows)

                for t_ in range(5):
                    def step(t_=t_, s_step=s_step, pv_step=pv_step):
                        if t_ < 4:
                            s_step(t_)
                        if t_ >= 1:
                            pv_step(t_ - 1)
                    defer(step, 1 + t_)

            if j + 1 < NTILE:
                xprep_slots(j + 1)[0]()
            if debug and j == 0:
                flush()
                dma("sp", dbg_mix, mixT[:, :, :].rearrange("p a b -> p (a b)"), mix_T, (), "cst")
                dma("sp", dbg_xT, xT[:, :, :].rearrange("p a b -> p (a b)"), xT_T, (), "cst")
                dma("sp", dbg_kT, kT[:, :, :].rearrange("p a b -> p (a b)"),
                    [t for p_ in kT_T for t in p_], (), "cst")
                dma("sp", dbg_v, v_sb[:, :, :].rearrange("p a b -> p (a b)"),
                    [t for p_ in v_T for t in p_], (), "cst")
                dma("sp", dbg_q, qT[:, :, :, :].rearrange("p a b c -> p (a b c)"),
                    [t for p_ in q_T for t in p_], (), "cst")
            nslots = xprep_slots(j + 1)[1] if j + 1 < NTILE else [[] for _ in range(32)]
            slot_i = 0
            for cg in range(8):
                r0 = 128 + TT * j
                dma("pool", xres[:, :, :],
                    x_ext[r0:r0 + TT, cg * 512:(cg + 1) * 512].rearrange("(t p) c -> p t c", p=128),
                    [], [xres_T], "xres")
                obanks = [alloc_bank(hold=True) for _ in range(4)]
                for si, sub in enumerate((2, 3, 0, 1)):
                    if cg == 0 and sub == 0:
                        flush()
                    so = next_slab()
                    for tb in range(4):
                        b = obanks[tb]
                        fns = [mm(ps[b][:, :], mixT[:, sub * 8 + kcl, tb * 128:(tb + 1) * 128],
                                  wring[so][:, kcl * 512:(kcl + 1) * 512], si == 0 and kcl == 0,
                                  si == 3 and kcl == 7) for kcl in range(8)]
                        S.mm_group(fns, [slot_T[so]] + mix_T[sub * 8:sub * 8 + 8],
                                   [bank_T[b]] if si == 0 else (), () if si == 0 else [bank_T[b]])
                    main_done()
                    prefetch()
                    for st in nslots[slot_i]:
                        st()
                    slot_i += 1
                for tb in range(4):
                    b = obanks[tb]
                    tt("dve", xres[:, tb, :], ps[b][:, :], xres[:, tb, :], ALU.add, [bank_T[b], xres_T],
                       (), [xres_T])
                    busy[b] = False
                o0 = TT * j
                dma("pool", out_d[o0:o0 + TT, cg * 512:(cg + 1) * 512].rearrange("(t p) c -> p t c", p=128),
                    xres[:, :, :], [xres_T], [out_T], "ost")
                store_count += 1
        flush()
        S.emit("pool", None, {"ost": S.cnt["ost"], "cst": S.cnt["cst"]})

        sem = S.sem

        def replay(eng_name, e):
            for waits, fn, sig, inc in S.streams[eng_name]:
                for s, v in waits:
                    e.wait_ge(sem[s], v)
                if fn is None:
                    continue
                ins = fn(e)
                if sig is not None:
                    ins.then_inc(sem[sig], inc)

        with nc.Block() as block:
            @block.tensor
            def _(e):
                replay("pe", e)

            @block.scalar
            def _(e):
                replay("act", e)

            @block.vector
            def _(e):
                replay("dve", e)

            @block.gpsimd
            def _(e):
                replay("pool", e)

            @block.sync
            def _(e):
                replay("sp", e)
    return nc


def _layout_weights(w_in, pw_w, w_out):
    w_in = np.asarray(w_in, dtype=np.float32).reshape(32, 128, 88, 128)
    wg = np.ascontiguousarray(w_in.transpose(2, 1, 0, 3)).reshape(88, 128, 4096)
    pw = np.asarray(pw_w, dtype=np.float32).reshape(16, 128, 8, 2, 128)
    pwr = np.ascontiguousarray(pw.transpose(2, 1, 3, 0, 4)).reshape(8, 128, 4096)
    wo = np.asarray(w_out, dtype=np.float32).reshape(4, 8, 128, 8, 512)
    wor = np.ascontiguousarray(wo.transpose(3, 0, 2, 1, 4)).reshape(8, 4, 128, 4096)
    slabs = []
    for i in range(16):
        slabs += [wg[40 + i], wg[56 + i]]
    for g in range(4):
        slabs.append(wg[20 + g])
    for kh in range(4):
        slabs.append(wg[16 + kh])
    for k in range(8):
        slabs += [wg[72 + 2 * k], wg[72 + 2 * k + 1], pwr[k]]
    for kh in range(4):
        for hh in range(4):
            slabs.append(wg[4 * kh + hh])
        for hh in range(4):
            slabs.append(wg[24 + 4 * kh + hh])
    for cg in range(8):
        for sub in (2, 3, 0, 1):
            slabs.append(wor[cg, sub])
    return np.stack(slabs, axis=0)


def _consts(half, norm_gain, q_norm_gain, k_norm_gain, attn_sink, conv_dw_w, conv_dw_b,
            conv_ln_gain, conv_ln_bias, conv_pw_b):
    c = np.zeros((128, NCST), np.float32)
    c[:, C_ID:C_ID + 128] = np.eye(128, dtype=np.float32)
    kk = np.arange(128)[:, None]
    qq = np.arange(128)[None, :]
    m_prev = (kk >= qq).astype(np.float32)
    m_next = (kk <= qq).astype(np.float32)
    c[:, C_MASK + 0:C_MASK + 128] = m_prev
    c[:, C_MASK + 128:C_MASK + 256] = m_next
    c[:, C_MASK + 256:C_MASK + 384] = m_prev if half == 1 else 0.0
    c[:, C_MASK + 384:C_MASK + 512] = m_next if half == 0 else 0.0
    sw = np.zeros((32, 32), np.float32)
    for m in range(32):
        sw[(m + 16) % 32, m] = 1.0
    c[0:32, C_SW:C_SW + 32] = sw
    c[:, C_GAIN:C_GAIN + 32] = np.asarray(norm_gain, np.float32).reshape(32, 128).T
    c[:, C_QKG] = np.asarray(q_norm_gain, np.float32).reshape(128)
    c[:, C_QKG + 1] = np.asarray(k_norm_gain, np.float32).reshape(128)
    c[:, C_SINK:C_SINK + 16] = np.asarray(attn_sink, np.float32).reshape(1, 16)
    dw = np.asarray(conv_dw_w, np.float32).reshape(31, 16, 128)
    c[:, C_DW:C_DW + 496] = dw.transpose(2, 1, 0).reshape(128, 496)
    c[:, C_DWB:C_DWB + 16] = np.asarray(conv_dw_b, np.float32).reshape(16, 128).T
    c[:, C_LNG:C_LNG + 16] = np.asarray(conv_ln_gain, np.float32).reshape(16, 128).T
    c[:, C_LNB:C_LNB + 16] = np.asarray(conv_ln_bias, np.float32).reshape(16, 128).T
    c[:, C_PWB:C_PWB + 16] = np.asarray(conv_pw_b, np.float32).reshape(16, 128).T
    c[:, C_EPS] = NORM_EPS
    c[:, C_EPS + 1] = LN_EPS
    return c


def _rope_table(half):
    pos = (half * TOK - 128 + np.arange(EXT)).astype(np.float32)
    inv = np.power(np.float32(ROPE_THETA), -np.arange(16, dtype=np.float32) * np.float32(2.0 / 32.0))
    ang = pos[None, :] * inv[:, None]
    cos = np.cos(ang).astype(np.float32)
    sin = np.sin(ang).astype(np.float32)
    t = np.zeros((32, 2, EXT), np.float32)
    t[0:16, 0] = cos
    t[16:32, 0] = cos
    t[0:16, 1] = -sin
    t[16:32, 1] = sin
    return t


_NC_CACHE = {}


def kernel(x, norm_gain, w_in, q_norm_gain, k_norm_gain, attn_sink, conv_dw_w, conv_dw_b,
           conv_ln_gain, conv_ln_bias, conv_pw_w, conv_pw_b, w_out):
    x = np.asarray(x, dtype=np.float32)
    wall = _layout_weights(np.asarray(w_in)[0], np.asarray(conv_pw_w)[0], np.asarray(w_out)[0])
    ropes = [_rope_table(0), _rope_table(1)]
    csts = [_consts(h, np.asarray(norm_gain)[0], np.asarray(q_norm_gain)[0], np.asarray(k_norm_gain)[0],
                    np.asarray(attn_sink)[0], np.asarray(conv_dw_w)[0], np.asarray(conv_dw_b)[0],
                    np.asarray(conv_ln_gain)[0], np.asarray(conv_ln_bias)[0], np.asarray(conv_pw_b)[0])
            for h in (0, 1)]
    in_maps = []
    for c in range(NCORES):
        b, half = c // 2, c % 2
        xe = np.zeros((EXT, D_MODEL), np.float32)
        lo = half * TOK - 128
        s0, s1 = max(lo, 0), min(lo + EXT, SEQ)
        xe[s0 - lo:s1 - lo] = x[b, s0:s1]
        in_maps.append({"x_ext": xe, "wall": wall, "cst": csts[half], "rope": ropes[half]})
    if "nc" not in _NC_CACHE:
        _NC_CACHE["nc"] = build_program()
    res = run_bass_kernel_spmd(_NC_CACHE["nc"], in_maps, core_ids=list(range(NCORES)))
    out = np.empty((BATCH, SEQ, D_MODEL), np.float32)
    for c in range(NCORES):
        b, half = c // 2, c % 2
        out[b, half * TOK:(half + 1) * TOK] = np.asarray(res.results[c]["out"], dtype=np.float32)
    return out
```
